# Optimizing a Trainium2 kernel written in Bass

```python
import jax, jax.numpy as jnp
from jax import lax
import numpy as np

D_MODEL = 1024
BATCH = 32
SEQ = 2048
DEPTH = 2

N_BRANCH = 4
BRANCH_W = D_MODEL // 2
N_GROUPS = 4
GROUP_W = BRANCH_W // N_GROUPS
POOL_WINDOWS = (2, 4, 8, 16)
CONV_K = 31
SHORT_K = 3
CHUNK = 128
N_PIECES = 12
N_BRANCH_COLS = N_PIECES * BRANCH_W
IN_COLS = N_BRANCH_COLS + N_BRANCH * D_MODEL
RMS_EPS = 1e-6
LN_EPS = 1e-5

kernel_name = "hybrid_gated_parallel_mixers"


def rms_norm(x, g):
    xf = x.astype(jnp.float32)
    y = xf * lax.rsqrt(jnp.mean(xf * xf, axis=-1, keepdims=True) + RMS_EPS)
    return (y * g.astype(jnp.float32)).astype(x.dtype)


def layer_norm(x, g, b):
    xf = x.astype(jnp.float32)
    mu = jnp.mean(xf, axis=-1, keepdims=True)
    var = jnp.mean(jnp.square(xf - mu), axis=-1, keepdims=True)
    y = (xf - mu) * lax.rsqrt(var + LN_EPS)
    return (y * g.astype(jnp.float32) + b.astype(jnp.float32)).astype(x.dtype)


def causal_dwconv(x, w):
    k, c = w.shape
    return lax.conv_general_dilated(
        x, w[:, None, :].astype(x.dtype), window_strides=(1,), padding=[(k - 1, 0)],
        dimension_numbers=("NWC", "WIO", "NWC"), feature_group_count=c)


def pool_mixer(xa, pool_w, pool_scale):
    b, s, _ = xa.shape
    xf = xa.astype(jnp.float32)
    csum = jnp.cumsum(xf, axis=1)
    t = jnp.arange(1, s + 1, dtype=jnp.float32)
    groups = []
    for j, win in enumerate(POOL_WINDOWS):
        cj = csum[..., j * GROUP_W:(j + 1) * GROUP_W]
        prev = jnp.pad(cj, ((0, 0), (win, 0), (0, 0)))[:, :s]
        mean = (cj - prev) / jnp.minimum(t, float(win))[None, :, None]
        groups.append(mean - xf[..., j * GROUP_W:(j + 1) * GROUP_W])
    pooled = jnp.stack(groups, axis=2)
    mixed = jnp.einsum("bsgc,gcd->bsgd", pooled, pool_w.astype(jnp.float32))
    return (mixed.reshape(b, s, BRANCH_W) * pool_scale.astype(jnp.float32)).astype(xa.dtype)


def conformer_conv(a, gb, conv_w, conv_b, ln_g, ln_b):
    y = a * jax.nn.sigmoid(gb)
    y = causal_dwconv(y, conv_w) + conv_b.astype(y.dtype)
    return jax.nn.silu(layer_norm(y, ln_g, ln_b))


def spatial_gating(u, v, ln_g, ln_b, sgu_w, sgu_b):
    b, s, _ = u.shape
    v = layer_norm(v, ln_g, ln_b).reshape(b, s // CHUNK, CHUNK, N_GROUPS, GROUP_W)
    mask = jnp.tril(jnp.ones((CHUNK, CHUNK), dtype=v.dtype))
    ws = sgu_w.astype(v.dtype) * mask[None]
    sp = jnp.einsum("gts,bnsgc->bntgc", ws, v) + sgu_b.T.astype(v.dtype)[None, None, :, :, None]
    return u * sp.reshape(b, s, BRANCH_W)


def short_gated_conv(bg, cg, xs, sc_w):
    return bg * causal_dwconv(cg * xs, sc_w)


def setup_inputs(seed: int = 0) -> dict:
    key = jax.random.key(seed)
    ks = jax.random.split(key, 20)
    f32 = jnp.float32
    nrm = lambda k, shape, scale: jax.random.normal(k, shape, f32) * scale
    return {
        "x": jax.random.normal(ks[0], (BATCH, SEQ, D_MODEL), f32),
        "norm_g": 1.0 + nrm(ks[1], (DEPTH, D_MODEL), 0.02),
        "w_in": nrm(ks[2], (DEPTH, D_MODEL, IN_COLS), D_MODEL ** -0.5),
        "pool_w": nrm(ks[3], (DEPTH, N_GROUPS, GROUP_W, GROUP_W), GROUP_W ** -0.5),
        "pool_scale": 1.0 + nrm(ks[4], (DEPTH, BRANCH_W), 0.02),
        "conv_w": nrm(ks[5], (DEPTH, CONV_K, BRANCH_W), CONV_K ** -0.5),
        "conv_b": nrm(ks[6], (DEPTH, BRANCH_W), 0.01),
        "conv_ln_g": 1.0 + nrm(ks[7], (DEPTH, BRANCH_W), 0.02),
        "conv_ln_b": nrm(ks[8], (DEPTH, BRANCH_W), 0.01),
        "sgu_ln_g": 1.0 + nrm(ks[9], (DEPTH, BRANCH_W), 0.02),
        "sgu_ln_b": nrm(ks[10], (DEPTH, BRANCH_W), 0.01),
        "sgu_w": nrm(ks[11], (DEPTH, N_GROUPS, CHUNK, CHUNK), CHUNK ** -0.5),
        "sgu_b": 1.0 + nrm(ks[12], (DEPTH, N_GROUPS, CHUNK), 0.01),
        "sc_w": nrm(ks[13], (DEPTH, SHORT_K, BRANCH_W), SHORT_K ** -0.5),
        "w_branch": nrm(ks[14], (DEPTH, N_BRANCH, BRANCH_W, D_MODEL), BRANCH_W ** -0.5),
        "w_o": nrm(ks[15], (DEPTH, D_MODEL, D_MODEL), D_MODEL ** -0.5),
        "final_g": 1.0 + nrm(ks[16], (D_MODEL,), 0.02),
    }


def reference(x, norm_g, w_in, pool_w, pool_scale, conv_w, conv_b, conv_ln_g, conv_ln_b,
              sgu_ln_g, sgu_ln_b, sgu_w, sgu_b, sc_w, w_branch, w_o, final_g):
    b, s, d = x.shape
    for l in range(DEPTH):
        h = rms_norm(x, norm_g[l])
        proj = jnp.einsum("bsd,dk->bsk", h, w_in[l].astype(h.dtype))
        (p_x, p_gate, c_a, c_b, c_gate, g_u, g_v, g_gate,
         s_b, s_c, s_x, s_gate) = jnp.split(proj[..., :N_BRANCH_COLS], N_PIECES, axis=-1)
        merge_gates = jax.nn.sigmoid(proj[..., N_BRANCH_COLS:].reshape(b, s, N_BRANCH, d))

        z_pool = pool_mixer(p_x, pool_w[l], pool_scale[l]) * jax.nn.silu(p_gate)
        z_conv = conformer_conv(c_a, c_b, conv_w[l], conv_b[l], conv_ln_g[l], conv_ln_b[l]) * jax.nn.silu(c_gate)
        z_sgu = spatial_gating(g_u, g_v, sgu_ln_g[l], sgu_ln_b[l], sgu_w[l], sgu_b[l]) * jax.nn.silu(g_gate)
        z_sc = short_gated_conv(s_b, s_c, s_x, sc_w[l]) * jax.nn.silu(s_gate)

        z = jnp.stack([z_pool, z_conv, z_sgu, z_sc], axis=2)
        branch_out = jnp.einsum("bsnc,ncd->bsnd", z, w_branch[l].astype(z.dtype))
        merged = jnp.sum(merge_gates * branch_out, axis=2)
        x = x + jnp.einsum("bsd,de->bse", merged, w_o[l].astype(merged.dtype))
    return rms_norm(x, final_g)
```

```python
import numpy as np
from contextlib import ExitStack
import concourse.bass as bass
import concourse.mybir as mybir
from concourse.bass_utils import run_bass_kernel_spmd

F32 = mybir.dt.float32
BF16 = mybir.dt.bfloat16
AF = mybir.ActivationFunctionType
ALU = mybir.AluOpType

D = 1024
KC = 8
T = 1024
NT = 2
TS = 512
NCORE = 8
SEQ = 2048
NSLAB = 52
RING = 6
CONV_K = 31
HALO = 32
RMS_EPS = 1e-6
LN_EPS = 1e-5
PP_NORMG, PP_PSCALE, PP_CONVB, PP_CLNG, PP_CLNB, PP_CONVW, PP_SCW = 0, 8, 12, 16, 20, 24, 148
PP_L = 160
PP_FINAL = 2 * PP_L
NPP = 2 * PP_L + 8
PIECE_ORDER = [0, 1, 2, 3, 4, 6, 7, 5, 9, 10, 8, 11]
PIECE_POS = {p: i for i, p in enumerate(PIECE_ORDER)}
DEBUG = False
DBG_OUT = {}


class Buf:
    __slots__ = ("w", "r", "name")

    def __init__(self, name=""):
        self.w = None
        self.r = []
        self.name = name


class Sched:
    ENGS = ("pe", "act", "dve", "pool", "sp")

    def __init__(self):
        self.q = {e: [] for e in self.ENGS}
        self.cnt = {}
        self.seen = {e: {} for e in self.ENGS}
        self.keys = set(self.ENGS)

    def _emit(self, eng, deps, fn, inc_key, inc_amt):
        need = {}
        for tok in deps:
            if tok is None:
                continue
            k, c, clk = tok
            if eng == "pe" and k == "pe":
                continue
            if need.get(k, (0, None))[0] < c:
                need[k] = (c, clk)
        waits = []
        seen = self.seen[eng]
        for k, (c, clk) in need.items():
            if seen.get(k, 0) >= c:
                continue
            waits.append((k, c))
            seen[k] = c
            if clk:
                for k2, c2 in clk.items():
                    if seen.get(k2, 0) < c2:
                        seen[k2] = c2
        tok = None
        if inc_key is not None:
            self.keys.add(inc_key)
            self.cnt[inc_key] = self.cnt.get(inc_key, 0) + inc_amt
            clk = dict(seen)
            if inc_key == eng:
                clk[eng] = self.cnt[inc_key] - 1
            tok = (inc_key, self.cnt[inc_key], clk)
        self.q[eng].append((waits, fn, inc_key, inc_amt))
        return tok

    def op(self, eng, fn, reads=(), writes=(), dma_key=None):
        deps = []
        for b in reads:
            deps.append(b.w)
        for b in writes:
            deps.append(b.w)
            deps.extend(b.r)
        if dma_key is None:
            tok = self._emit(eng, deps, fn, eng, 1)
        else:
            tok = self._emit(eng, deps, fn, dma_key, 16)
        for b in reads:
            b.r.append(tok)
        for b in writes:
            b.w = tok
            b.r = []
        return tok

    def barrier(self):
        toks = []
        for k in list(self.keys):
            c = self.cnt.get(k, 0)
            if c > 0:
                toks.append((k, c, None))
        for e in ("pe", "act", "dve", "pool", "sp"):
            self._emit(e, toks, None, None, 0)

    def final_wait(self, eng, toks):
        self._emit(eng, toks, None, None, 0)


def _build(layers, final, n_units):
    nc = bass.Bass("TRN2", target_bir_lowering=False)
    NTOK = n_units * T
    xT_d = nc.dram_tensor("xT", [D, NTOK], F32, kind="ExternalInput").ap()
    out_d = nc.dram_tensor("outT", [D, NTOK], F32, kind="ExternalOutput").ap()
    w_in_d = nc.dram_tensor("w_in", [2, D, 10240], F32, kind="ExternalInput").ap()
    w_br_d = nc.dram_tensor("w_branch", [2, 4, 512, 1024], F32, kind="ExternalInput").ap()
    w_o_d = nc.dram_tensor("w_o", [2, 1024, 1024], F32, kind="ExternalInput").ap()
    pp_d = nc.dram_tensor("pp", [128, NPP], F32, kind="ExternalInput").ap()
    bc_d = nc.dram_tensor("bc", [128, 4 * 512], F32, kind="ExternalInput").ap()
    poolw_d = nc.dram_tensor("poolw", [128, 8 * 128], F32, kind="ExternalInput").ap()
    sguw_d = nc.dram_tensor("sguw", [128, 8 * 128], F32, kind="ExternalInput").ap()
    sgub_d = nc.dram_tensor("sgub", [1, 1024], F32, kind="ExternalInput").ap()
    cst_d = nc.dram_tensor("cst", [128, 128 + 128 + 64], F32, kind="ExternalInput").ap()
    wsc_d = nc.dram_tensor("wscratch", [2, NSLAB, 128, 2048], BF16, kind="Internal").ap()
    if DEBUG:
        dbgz_d = nc.dram_tensor("dbgz", [128, 16 * T], BF16, kind="ExternalOutput").ap()
        dbgh_d = nc.dram_tensor("dbgh", [128, 8 * T], BF16, kind="ExternalOutput").ap()
        dbg1_d = nc.dram_tensor("dbg1", [128, 8], F32, kind="ExternalOutput").ap()
        dbg2_d = nc.dram_tensor("dbg2", [128, 16], F32, kind="ExternalOutput").ap()
        dbg3_d = nc.dram_tensor("dbg3", [128, 64], BF16, kind="ExternalOutput").ap()
        dbg4_d = nc.dram_tensor("dbg4", [128, 64], F32, kind="ExternalOutput").ap()

    S = Sched()
    es = ExitStack()

    def sb(name, shape, dt):
        return es.enter_context(nc.sbuf_tensor("sb_" + name, shape, dt))

    with es:
        xT = sb("xT", [128, KC, T], F32)
        hT = sb("hT", [128, KC, T], BF16)
        zT = sb("zT", [128, 16, T], BF16)
        RG = sb("RG", [128, 17792], BF16)
        BB = sb("BB", [128, 4, HALO + T], BF16)
        HS = sb("HS", [128, 2 * 3 * 4, HALO], BF16)
        ring = sb("ring", [128, RING, 2048], BF16)
        pp = sb("pp", [128, NPP], F32)
        wh = sb("wh", [128, 2, 136], F32)
        bc = sb("bc", [128, 4, 512], F32)
        PW = sb("PW", [128, 8, 3, 128], BF16)
        wsT = sb("wsT", [128, 8, 128], BF16)
        SBm = sb("SBm", [128, 1024], BF16)
        E01 = sb("E01", [128, 128], BF16)
        cst = sb("cst", [128, 320], F32)
        ident = sb("ident", [128, 128], BF16)
        onesA = sb("onesA", [128, 128], BF16)
        onesC = sb("onesC", [128, 128], BF16)
        ES = sb("ES", [128, 2, 32], F32)
        cm05 = sb("cm05", [128, 8], F32)
        corr = sb("corr", [128, 4, 16], BF16)
        sq = sb("sq", [128, 2, TS], BF16)
        st = sb("st", [128, 2, TS], F32)
        stgS = sb("stgS", [128, 2, 2048], BF16)
        at = sb("at", [128, 2, TS], F32)
        dt_ = sb("dt", [128, 2, TS], F32)
        yc = sb("yc", [128, 4, TS], F32)
        poolw_f = yc[:, 0:2, :].rearrange("p a (b n) -> p (a b) n", n=128)
        sguw_f = yc[:, 2:4, :].rearrange("p a (b n) -> p (a b) n", n=128)
        sgub_f = at[0:1, 0:2, :].rearrange("p a n -> p (a n)")
        sgub_hf = st[0:1, 0:2, :].rearrange("p a n -> p (a n)")
        sgub_hi = sq[0:1, 0:2, :].rearrange("p a n -> p (a n)")
        vn = sb("vn", [128, 2, TS], F32)
        vg = sb("vg", [128, 2, TS], BF16)
        sgub_lo = vg[0:1, 0:2, :].rearrange("p a n -> p (a n)")
        bnst = sb("bnst", [128, 2, 8], F32)
        bnmv = sb("bnmv", [128, 2, 4], F32)
        ps = es.enter_context(nc.psum_tensor("ps", [128, 8, TS], F32))

        sem_names = list(Sched.ENGS) + ["cst%d" % i for i in range(8)] + \
            ["slot%d" % i for i in range(RING)] + ["misc", "os0", "os1"] + ["xl%d%d" % (a_, b_) for a_ in range(2) for b_ in range(4)] + ["cv%d" % i for i in range(10)] + ["cs%d" % i for i in range(10)]
        sems = {k: es.enter_context(nc.semaphore("s_" + k)) for k in sem_names}

        xb = [[Buf() for _ in range(NT)] for _ in range(KC)]
        hb = [[Buf() for _ in range(NT)] for _ in range(KC)]
        zb = [[Buf() for _ in range(NT)] for _ in range(16)]
        mb = [[Buf() for _ in range(NT)] for _ in range(KC)]
        bbb = [[Buf() for _ in range(NT)] for _ in range(4)]
        bbp = [Buf() for _ in range(4)]
        hsb = [Buf() for _ in range(6)]
        diagb = [Buf() for _ in range(4)]
        scdb = Buf()
        slotb = [Buf() for _ in range(RING)]
        bankb = [Buf() for _ in range(8)]
        cb = Buf("consts")
        esb = [Buf(), Buf()]
        corrb = [Buf() for _ in range(4)]

        class Ring:
            def __init__(self, t, n):
                self.t, self.n, self.i = t, n, 0
                self.b = [Buf() for _ in range(n)]

            def get(self):
                i = self.i
                self.i = (i + 1) % self.n
                return self.t[:, i, :], self.b[i]

        sq_r, st_r, at_r, dt_r = Ring(sq, 2), Ring(st, 2), Ring(at, 2), Ring(dt_, 2)
        vn_r, vg_r = Ring(vn, 2), Ring(vg, 2)
        ycb_r, ysq_r = vg_r, sq_r
        ycbuf = [Buf() for _ in range(4)]
        bn_r = [Buf(), Buf()]
        bn_i = [0]

        diag = RG[:, 0:124 * 128].rearrange("p (j m) -> p j m", m=128)
        scd = RG[:, 15872:15872 + 12 * 128].rearrange("p (j m) -> p j m", m=128)
        mT = RG[:, 0:8192].rearrange("p (k t) -> p k t", t=T)
        accp = RG[:, 8192:8192 + 4096].bitcast(F32).rearrange("p (k t) -> p k t", t=TS)
        acc_r = Ring(accp[:, 0:2, :], 2)
        p_r = Ring(accp[:, 2:4, :], 2)

        held = set()
        bank_i = [0]

        def bank(hold=False):
            while True:
                i = bank_i[0]
                bank_i[0] = (i + 1) % 8
                if i not in held:
                    break
            if hold:
                held.add(i)
            return ps[:, i, :], bankb[i], i

        def mm(mms, reads, writes, flags=None):
            n = len(mms)

            def fn(e):
                ins = None
                for i, (o, l, r) in enumerate(mms):
                    if flags is None:
                        s0, s1 = (i == 0), (i == n - 1)
                    else:
                        s0, s1 = flags[i]
                    ins = e.matmul(o, l, r, start=s0, stop=s1)
                return ins
            return S.op("pe", fn, reads, writes)

        def act(out, in_, func, reads, writes, scale=1.0, bias=0.0):
            return S.op("act", lambda e: e.activation(out, in_, func, bias=bias, scale=scale), reads, writes)

        def dve(fn, reads, writes):
            return S.op("dve", fn, reads, writes)

        def tt_(out, a, b, op, reads, writes, eng="dve"):
            return S.op(eng, lambda e: e.tensor_tensor(out, a, b, op), reads, writes)

        def stt(out, in0, scalar, in1, op0, op1, reads, writes, eng="dve"):
            return S.op(eng, lambda e: e.scalar_tensor_tensor(out, in0, scalar, in1, op0, op1), reads, writes)

        def ts_(out, in0, s1, s2, op0, op1, reads, writes, eng="dve"):
            if s2 is None:
                return S.op(eng, lambda e: e.tensor_scalar(out, in0, s1, None, op0), reads, writes)
            return S.op(eng, lambda e: e.tensor_scalar(out, in0, s1, s2, op0, op1), reads, writes)

        def dma(eng, out, in_, reads, writes, key):
            return S.op(eng, lambda e: e.dma_start(out=out, in_=in_), reads, writes, dma_key=key)

        dma("sp", pp[:, :], pp_d, [], [cb], "cst0")
        dma("sp", bc[:, :, :], bc_d.rearrange("p (a n) -> p a n", n=512), [], [cb], "cst1")
        dma("sp", poolw_f, poolw_d.rearrange("p (a n) -> p a n", n=128), [], [cb], "cst2")
        dma("sp", sguw_f, sguw_d.rearrange("p (a n) -> p a n", n=128), [], [cb], "cst3")
        dma("sp", sgub_f, sgub_d, [], [cb], "cst4")
        dma("sp", cst[:, :], cst_d, [], [cb], "cst5")
        c2 = Buf()
        dve(lambda e: e.tensor_copy(ident[:, :], cst[:, 0:128]), [cb], [c2])
        dve(lambda e: e.memset(onesA[:, :], 1.0 / 1024.0), [], [c2])
        dve(lambda e: e.memset(onesC[:, :], 1.0 / 512.0), [], [c2])
        dve(lambda e: e.memset(E01[:, :], 0.0), [], [c2])
        dve(lambda e: e.memset(cm05[:, :], -0.5), [], [c2])
        dve(lambda e: e.memset(SBm[:, :], 0.0), [], [c2])
        dve(lambda e: e.memset(ES[:, :, :], 0.0), [], [esb[0], esb[1]])
        dve(lambda e: e.memset(BB[:, :, :], 0.0), [], [bbp[c] for c in range(4)])
        c3 = Buf()
        dve(lambda e: e.memset(E01[0:2, :], 1.0), [c2], [c3])
        for l in range(2):
            ts_(wh[:, l, 0:124], pp[:, l * PP_L + PP_CONVW: l * PP_L + PP_CONVW + 124], 0.5, None, ALU.mult, None, [cb], [c3])
            dve(lambda e, l=l: e.tensor_copy(wh[:, l, 124:136], pp[:, l * PP_L + PP_SCW: l * PP_L + PP_SCW + 12]), [cb], [c3])
            for g in range(4):
                win = float(2 ** (g + 1))
                j = l * 4 + g
                dve(lambda e, j=j: e.tensor_copy(PW[:, j, 0, :], poolw_f[:, j, :]), [cb], [c3])
                ts_(PW[:, j, 1, :], poolw_f[:, j, :], 1.0 / win - 1.0, None, ALU.mult, None, [cb], [c3])
                ts_(PW[:, j, 2, :], poolw_f[:, j, :], 1.0 / win, None, ALU.mult, None, [cb], [c3])
                tt_(wsT[:, j, :], sguw_f[:, j, :], cst[:, 128:256], ALU.mult, [cb], [c3])
        c4 = Buf()
        dve(lambda e: e.tensor_copy(sgub_hi, sgub_f), [cb], [c4])
        c5 = Buf()
        dve(lambda e: e.tensor_copy(sgub_hf, sgub_hi), [c4], [c5])
        c6 = Buf()
        tt_(sgub_lo, sgub_f, sgub_hf, ALU.subtract, [cb, c5], [c6])
        dma("sp", SBm[0:1, :], sgub_hi, [c4, c2], [c3], "cst6")
        dma("sp", SBm[1:2, :], sgub_lo, [c6, c2], [c3], "cst7")
        constb = [cb, c2, c3]

        wscbufs = [[Buf() for _ in range(NSLAB)] for _ in range(2)]

        def convert_slab(l, s, stg_ap, stg_buf, kc_, ks_, store_eng):
            win_v = w_in_d[l].rearrange("(k p) n -> p k n", p=128)
            if s < 24:
                piece, half = PIECE_ORDER[s // 2], s % 2
                c0 = piece * 512 + half * 256
                parts = [(stg_ap.rearrange("p (k j) -> p k j", j=256), win_v[:, :, c0:c0 + 256])]
            elif s < 48:
                dc, r = (s - 24) // 3, (s - 24) % 3
                parts = []
                if r < 2:
                    dv = stg_ap.rearrange("p (k a j) -> p k a j", a=2, j=128)
                    for a_ in range(2):
                        n = 2 * r + a_
                        c0 = 6144 + n * 1024 + dc * 128
                        parts.append((dv[:, :, a_, :], win_v[:, :, c0:c0 + 128]))
                else:
                    dv = stg_ap.rearrange("p (a k j) -> p a k j", a=4, j=128)
                    for n in range(4):
                        src = w_br_d[l, n].rearrange("(k p) d -> p k d", p=128)
                        parts.append((dv[:, n, :, :], src[:, :, dc * 128:(dc + 1) * 128]))
            else:
                ecp = s - 48
                src = w_o_d[l].rearrange("(k p) e -> p k e", p=128)
                parts = [(stg_ap.rearrange("p (k j) -> p k j", j=256), src[:, :, ecp * 256:(ecp + 1) * 256])]
            for d_, s_ in parts:
                dma("pool", d_, s_, [], [stg_buf], kc_)
            if store_eng is not None:
                dma(store_eng, wsc_d[l, s], stg_ap, [stg_buf], [wscbufs[l][s]], ks_)

        NB = 3
        pstg = [zT[:, 2 * j:2 * j + 2, :].rearrange("p a t -> p (a t)") for j in range(NB)]
        pstgb = [Buf() for _ in range(NB)]
        NPRO = 12
        for s_i in range(NPRO):
            j = s_i % NB
            convert_slab(layers[0], s_i, pstg[j], pstgb[j], "cv%d" % j, "cs%d" % j, "sp")
        S.barrier()

        bgq = [(layers[0], s_i) for s_i in range(NPRO, NSLAB)] + [(l, s_i) for l in layers[1:] for s_i in range(NSLAB)]
        unconverted = set(bgq)
        sstg = [stgS[:, 0, :], stgS[:, 1, :]]
        sstgb = [Buf(), Buf()]
        bg = {"n": 0, "calls": 0}

        def hook(force=False):
            bg["calls"] += 1
            if not bgq and not bg.get("pend"):
                return
            if not force and bg["calls"] % 4 != 0:
                return
            pend = bg.get("pend")
            if pend is not None:
                l_, s_, j = pend
                dma("sp", wsc_d[l_, s_], sstg[j], [sstgb[j]], [wscbufs[l_][s_]], "cs%d" % (8 + j))
                unconverted.discard((l_, s_))
                bg["pend"] = None
            if bgq:
                l_, s_ = bgq.pop(0)
                j = bg["n"] % 2
                bg["n"] += 1
                convert_slab(l_, s_, sstg[j], sstgb[j], "cv%d" % (8 + j), None, None)
                bg["pend"] = (l_, s_, j)

        seq = []
        for u in range(n_units):
            for l in layers:
                for s in range(NSLAB):
                    seq.append((l, s))
        st8 = {"loaded": 0, "released": 0, "base": 0}

        def slab_pump():
            while st8["loaded"] < len(seq) and st8["loaded"] < st8["released"] + RING:
                j = st8["loaded"]
                l, s = seq[j]
                slot = j % RING
                while (l, s) in unconverted:
                    hook(force=True)
                dma("sp", ring[:, slot, :], wsc_d[l, s], [wscbufs[l][s]], [slotb[slot]], "slot%d" % slot)
                st8["loaded"] += 1

        def slab(s):
            j = st8["base"] + s
            assert j < st8["loaded"], (j, st8)
            slot = j % RING
            return ring[:, slot, :], slotb[slot]

        def release(upto_s):
            r = st8["base"] + upto_s + 1
            if r > st8["released"]:
                st8["released"] = r
            slab_pump()

        slab_pump()

        def cols(tt):
            return slice(tt * TS, (tt + 1) * TS)

        def proj(s, sub, tt):
            hook()
            w, wbuf = slab(s)
            wv = w.rearrange("p (k j) -> p k j", j=256)
            bk, bb_, _ = bank()
            mm([(bk, wv[:, k, sub * 128:(sub + 1) * 128], hT[:, k, cols(tt)]) for k in range(KC)],
               [wbuf] + [hb[k][tt] for k in range(KC)], [bb_])
            return bk, bb_

        def pcol(l, base, c):
            o = l * PP_L + base + c
            return pp[:, o:o + 1]

        def rms_stats(tt):
            bk, bb_, _ = bank()
            for k in range(KC):
                s_ap, s_b = sq_r.get()
                act(s_ap, xT[:, k, cols(tt)], AF.Square, [xb[k][tt]], [s_b])
                mm([(bk, onesA[:, :], s_ap)], [s_b] + constb, [bb_], flags=[(k == 0, k == KC - 1)])
            r_ap, r_b = st_r.get()
            ts_(r_ap, bk, RMS_EPS, None, ALU.add, None, [bb_], [r_b])
            act(r_ap, r_ap, AF.Ln, [r_b], [r_b])
            act(r_ap, r_ap, AF.Exp, [r_b], [r_b], scale=-0.5)
            return r_ap, r_b

        def xload(u_, tt_i):
            xv = xT_d.rearrange("(k p) t -> p k t", p=128)[:, :, u_ * T + tt_i * TS:u_ * T + (tt_i + 1) * TS]
            for k0 in range(0, KC, 2):
                dma("pool", xT[:, k0:k0 + 2, cols(tt_i)], xv[:, k0:k0 + 2, :], [],
                    [xb[k][tt_i] for k in range(k0, k0 + 2)], "xl%d%d" % (tt_i, k0 // 2))

        def phase_a(l_, tt_i):
            r_ap, r_b = rms_stats(tt_i)
            for k in range(KC):
                stt(hT[:, k, cols(tt_i)], xT[:, k, cols(tt_i)], pcol(l_, PP_NORMG, k), r_ap, ALU.mult, ALU.mult,
                    [xb[k][tt_i], r_b] + constb, [hb[k][tt_i]])

        outv = zT[:, :, :].rearrange("p a t -> p (a t)").bitcast(F32).rearrange("p (k t) -> p k t", t=T)
        fence = sb("fence", [128, 8], F32)
        fenceb = Buf()
        rg_cd_bufs = [mb[k][tt] for k in range(KC) for tt in range(NT)] + acc_r.b + p_r.b

        out_toks = []
        for u in range(n_units):
            seq_start = (u % 2 == 0)
            if u == 0:
                for tt in range(NT):
                    xload(0, tt)
                    phase_a(layers[0], tt)
            for li, l in enumerate(layers):
                st8["base"] = (u * len(layers) + li) * NSLAB

                def load_halo(br, l=l, seq_start=seq_start):
                    for c in range(4):
                        d_ = BB[:, c, 0:HALO]
                        if seq_start:
                            S.op("pool", lambda e, d_=d_: e.memset(d_, 0.0), [], [bbp[c]])
                        else:
                            s_ = HS[:, (l * 3 + br) * 4 + c, :]
                            S.op("pool", lambda e, d_=d_, s_=s_: e.tensor_copy(d_, s_), [hsb[l * 3 + br]], [bbp[c]])

                def save_halo(br, l=l):
                    for c in range(4):
                        d_ = HS[:, (l * 3 + br) * 4 + c, :]
                        s_ = BB[:, c, T:T + HALO]
                        S.op("pool", lambda e, d_=d_, s_=s_: e.tensor_copy(d_, s_), [bbb[c][NT - 1]], [hsb[l * 3 + br]])

                load_halo(0)
                dq = []
                for c in range(4):
                    for k0, k1 in ((0, 8), (8, 16), (16, 24), (24, CONV_K)):
                        o_ = diag[:, c * CONV_K + k0:c * CONV_K + k1, :]
                        i0 = ident[:, :].unsqueeze(1).broadcast_to([128, k1 - k0, 128])
                        i1 = wh[:, l, c * CONV_K + k0:c * CONV_K + k1].unsqueeze(2).broadcast_to([128, k1 - k0, 128])
                        dq.append((o_, i0, i1, [diagb[c]] + (rg_cd_bufs if k0 == 0 else [])))
                o_ = scd[:, 0:12, :]
                i0 = ident[:, :].unsqueeze(1).broadcast_to([128, 12, 128])
                i1 = wh[:, l, 124:136].unsqueeze(2).broadcast_to([128, 12, 128])
                dq.append((o_, i0, i1, [scdb] + rg_cd_bufs))

                def diag_some(n):
                    for _ in range(n):
                        if dq:
                            o_, i0, i1, wr = dq.pop(0)
                            tt_(o_, i0, i1, ALU.mult, constb, wr)

                for tt in range(NT):
                    for c in range(4):
                        bk, bb_ = proj(c // 2, c % 2, tt)
                        act(BB[:, c, HALO + tt * TS:HALO + (tt + 1) * TS], bk, AF.Copy, [bb_], [bbb[c][tt]])
                release(1)
                for tt in range(NT):
                    for c in range(4):
                        win = 2 ** (c + 1)
                        j = l * 4 + c
                        rd = [bbb[c][tt], bbp[c] if tt == 0 else bbb[c][tt - 1]] + constb
                        mms = []
                        for sft in range(win):
                            o = HALO + tt * TS - sft
                            mms.append((None, PW[:, j, 1 if sft == 0 else 2, :], BB[:, c, o:o + TS]))
                        bk, bb_, _ = bank()
                        mms = [(bk, a, b) for (_, a, b) in mms]
                        flags = [(i == 0, i == len(mms) - 1) for i in range(len(mms))]
                        if seq_start and tt == 0:
                            cur = 0
                            tt_(ES[:, 0, 16:32], BB[:, c, HALO:HALO + 16], BB[:, c, HALO - 1:HALO + 15], ALU.add,
                                [bbb[c][0], bbp[c]], [esb[0]])
                            sh = 2
                            while sh < win:
                                tt_(ES[:, 1 - cur, 16:32], ES[:, cur, 16:32], ES[:, cur, 16 - sh:32 - sh], ALU.add,
                                    [esb[cur]], [esb[1 - cur]])
                                cur = 1 - cur
                                sh *= 2
                            tt_(corr[:, c, :], ES[:, cur, 16:32], cst[:, 256 + c * 16:256 + (c + 1) * 16], ALU.mult,
                                [esb[cur]] + constb, [corrb[c]])
                            mms.append((bk[:, 0:16], PW[:, j, 0, :], corr[:, c, :]))
                            flags = [(i == 0, False) for i in range(len(mms) - 1)] + [(False, True)]
                            rd = rd + [corrb[c]]
                        mm(mms, rd, [bb_], flags=flags)
                        gk, gb_ = proj(2 + c // 2, c % 2, tt)
                        a_ap, a_b = at_r.get()
                        act(a_ap, gk, AF.Silu, [gb_], [a_b])
                        stt(zT[:, 0 * 4 + c, cols(tt)], bk, pcol(l, PP_PSCALE, c), a_ap, ALU.mult, ALU.mult,
                            [bb_, a_b] + constb, [zb[c][tt]])
                        diag_some(1)
                release(3)
                save_halo(0)

                sgu_gens = [None]

                def sgu_tile(tt, l=l):
                    Sk = [bank(hold=True) for _ in range(4)]

                    def small(tb, v_ap, v_b):
                        for g in range(4):
                            o = Sk[g][0][:, tb * 128:(tb + 1) * 128]
                            j = l * 4 + g
                            mm([(o, v_ap[:, g * 128:(g + 1) * 128], wsT[:, j, :]),
                                (o, E01[:, :], SBm[:, j * 128:(j + 1) * 128])],
                               [v_b] + constb, [Sk[g][1]])

                    def vproj(tb):
                        w0, wb0 = slab(10)
                        w1, wb1 = slab(11)
                        bk, bb_, _ = bank()
                        t0 = tt * TS + tb * 128
                        mms, flags = [], []
                        for hh, w in enumerate((w0, w1)):
                            wv = w.rearrange("p (k j) -> p k j", j=256)
                            for k in range(KC):
                                mms.append((bk[:, hh * 256:(hh + 1) * 256], hT[:, k, t0:t0 + 128], wv[:, k, :]))
                                flags.append((k == 0, k == KC - 1))
                        mm(mms, [wb0, wb1] + [hb[k][tt] for k in range(KC)], [bb_], flags=flags)
                        i = bn_i[0]
                        bn_i[0] = 1 - i
                        dve(lambda e: e.bn_stats(bnst[:, i, 0:6], bk), [bb_], [bn_r[i]])
                        dve(lambda e: e.bn_aggr(bnmv[:, i, 0:2], bnst[:, i, 0:6]), [bn_r[i]], [bn_r[i]])
                        ts_(bnmv[:, i, 2:3], bnmv[:, i, 1:2], LN_EPS, None, ALU.add, None, [bn_r[i]], [bn_r[i]])
                        tt_(bnmv[:, i, 2:3], bnmv[:, i, 2:3], cm05[:, 0:1], ALU.pow, [bn_r[i]] + constb, [bn_r[i]], eng="pool")
                        stt(bnmv[:, i, 3:4], bnmv[:, i, 0:1], -1.0, bnmv[:, i, 2:3], ALU.mult, ALU.mult, [bn_r[i]], [bn_r[i]])
                        n_ap, n_b = vn_r.get()
                        act(n_ap, bk, AF.Identity, [bb_, bn_r[i]], [n_b], scale=bnmv[:, i, 2:3], bias=bnmv[:, i, 3:4])
                        tt_(n_ap, n_ap, bc[:, l * 2 + 0, :], ALU.mult, [n_b] + constb, [n_b])
                        g_ap, g_b = vg_r.get()
                        tt_(g_ap, n_ap, bc[:, l * 2 + 1, :], ALU.add, [n_b] + constb, [g_b])
                        return g_ap, g_b

                    v0 = vproj(0)
                    v1 = vproj(1)
                    yield
                    small(0, *v0)
                    v2 = vproj(2)
                    small(1, *v1)
                    v3 = vproj(3)
                    small(2, *v2)
                    if tt == NT - 1:
                        release(11)
                    for c in range(4):
                        gk, gb_ = proj(12 + c // 2, c % 2, tt)
                        a_ap, a_b = at_r.get()
                        act(a_ap, gk, AF.Silu, [gb_], [a_b])
                        uk, ub_ = proj(14 + c // 2, c % 2, tt)
                        d_ap, d_b = dt_r.get()
                        tt_(d_ap, uk, a_ap, ALU.mult, [ub_, a_b], [d_b])
                        if c == 0:
                            small(3, *v3)
                        tt_(zT[:, 8 + c, cols(tt)], Sk[c][0], d_ap, ALU.mult, [Sk[c][1], d_b], [zb[8 + c][tt]])
                    for g in range(4):
                        held.discard(Sk[g][2])

                load_halo(1)
                for tt in range(NT):
                    for c in range(4):
                        bk, bb_ = proj(6 + c // 2, c % 2, tt)
                        a_ap, a_b = at_r.get()
                        act(a_ap, bk, AF.Tanh, [bb_], [a_b], scale=0.5)
                        ak, ab_ = proj(4 + c // 2, c % 2, tt)
                        stt(BB[:, c, HALO + tt * TS:HALO + (tt + 1) * TS], a_ap, 1.0, ak, ALU.add, ALU.mult,
                            [a_b, ab_], [bbb[c][tt]])
                        diag_some(1)
                diag_some(100)
                release(7)
                for tt in range(NT):
                    mk, mbk, _ = bank()
                    qk, qbk, _ = bank()

                    def stats_mm(c, yb_ap, yb_b, ys_ap, ys_b, mk=mk, mbk=mbk, qk=qk, qbk=qbk):
                        mm([(mk, onesC[:, :], yb_ap)], [yb_b] + constb, [mbk], flags=[(c == 0, c == 3)])
                        mm([(qk, onesC[:, :], ys_ap)], [ys_b] + constb, [qbk], flags=[(c == 0, c == 3)])

                    pend_s = None
                    for c in range(4):
                        bk, bb_, _ = bank()
                        rd = [bbb[c][tt], bbp[c] if tt == 0 else bbb[c][tt - 1], diagb[c]]
                        mms = []
                        for k in range(CONV_K):
                            o = HALO + tt * TS - (CONV_K - 1) + k
                            mms.append((bk, diag[:, c * CONV_K + k, :], BB[:, c, o:o + TS]))
                        mm(mms, rd, [bb_])
                        act(yc[:, c, :], bk, AF.Identity, [bb_] + constb, [ycbuf[c]], bias=pcol(l, PP_CONVB, c))
                        yb_ap, yb_b = ycb_r.get()
                        act(yb_ap, bk, AF.Identity, [bb_] + constb, [yb_b], bias=pcol(l, PP_CONVB, c))
                        ys_ap, ys_b = ysq_r.get()
                        act(ys_ap, bk, AF.Square, [bb_] + constb, [ys_b], bias=pcol(l, PP_CONVB, c))
                        if pend_s is not None:
                            stats_mm(*pend_s)
                        pend_s = (c, yb_ap, yb_b, ys_ap, ys_b)
                    stats_mm(*pend_s)
                    m_ap, m_b = st_r.get()
                    v_ap, v_b = st_r.get()
                    dve(lambda e, m_ap=m_ap, mk=mk: e.tensor_copy(m_ap, mk), [mbk], [m_b])
                    tt_(v_ap, m_ap, m_ap, ALU.mult, [m_b], [v_b])
                    stt(v_ap, qk, LN_EPS, v_ap, ALU.add, ALU.subtract, [qbk, v_b], [v_b])
                    act(v_ap, v_ap, AF.Ln, [v_b], [v_b])
                    act(v_ap, v_ap, AF.Exp, [v_b], [v_b], scale=-0.5)
                    tt_(m_ap, m_ap, v_ap, ALU.mult, [m_b, v_b], [m_b])
                    if tt == NT - 1:
                        sgu_gens[0] = sgu_tile(0)
                        next(sgu_gens[0])

                    def tail_a(c):
                        tt_(yc[:, c, :], yc[:, c, :], v_ap, ALU.mult, [ycbuf[c], v_b], [ycbuf[c]])
                        tt_(yc[:, c, :], yc[:, c, :], m_ap, ALU.subtract, [ycbuf[c], m_b], [ycbuf[c]])
                        s_ap, s_b = dt_r.get()
                        act(s_ap, yc[:, c, :], AF.Silu, [ycbuf[c]] + constb, [s_b],
                            scale=pcol(l, PP_CLNG, c), bias=pcol(l, PP_CLNB, c))
                        return s_ap, s_b

                    def tail_b(c, s_ap, s_b):
                        gk, gb_ = proj(8 + c // 2, c % 2, tt)
                        g_ap, g_b = at_r.get()
                        act(g_ap, gk, AF.Silu, [gb_], [g_b])
                        tt_(zT[:, 4 + c, cols(tt)], s_ap, g_ap, ALU.mult, [s_b, g_b], [zb[4 + c][tt]])

                    pend_c = None
                    for c in range(4):
                        sa = tail_a(c)
                        if pend_c is not None:
                            tail_b(*pend_c)
                        pend_c = (c,) + sa
                    tail_b(*pend_c)
                release(9)
                save_halo(1)

                for tt in range(NT):
                    if tt == 0:
                        g_ = sgu_gens[0]
                    else:
                        g_ = sgu_tile(tt)
                        next(g_)
                    for _ in g_:
                        pass
                release(15)

                load_halo(2)
                for tt in range(NT):
                    for c in range(4):
                        ck, cb_ = proj(16 + c // 2, c % 2, tt)
                        a_ap, a_b = at_r.get()
                        act(a_ap, ck, AF.Copy, [cb_], [a_b])
                        xk, xb_ = proj(18 + c // 2, c % 2, tt)
                        tt_(BB[:, c, HALO + tt * TS:HALO + (tt + 1) * TS], xk, a_ap, ALU.mult, [xb_, a_b], [bbb[c][tt]])
                release(19)
                for tt in range(NT):
                    for c in range(4):
                        bk, bb_, _ = bank()
                        rd = [bbb[c][tt], bbp[c] if tt == 0 else bbb[c][tt - 1], scdb]
                        mms = []
                        for k in range(3):
                            o = HALO + tt * TS - 2 + k
                            mms.append((bk, scd[:, c * 3 + k, :], BB[:, c, o:o + TS]))
                        mm(mms, rd, [bb_])
                        gk, gb_ = proj(22 + c // 2, c % 2, tt)
                        a_ap, a_b = at_r.get()
                        act(a_ap, gk, AF.Silu, [gb_], [a_b])
                        sk, sb_ = proj(20 + c // 2, c % 2, tt)
                        d_ap, d_b = dt_r.get()
                        tt_(d_ap, sk, a_ap, ALU.mult, [sb_, a_b], [d_b])
                        tt_(zT[:, 12 + c, cols(tt)], bk, d_ap, ALU.mult, [bb_, d_b], [zb[12 + c][tt]])
                release(23)
                save_halo(2)
                dve(lambda e: e.memset(fence[:, 0:1], 0.0), [], [fenceb] + diagb + [scdb] + rg_cd_bufs)

                for dc in range(8):
                    wbs, wbb = slab(24 + dc * 3 + 2)
                    wbv = wbs.rearrange("p (a k j) -> p a k j", a=4, j=128)
                    for tt in range(NT):
                        acc_ap, acc_b = acc_r.get()
                        for n in range(4):
                            hook()
                            hook()
                            gs, gsb = slab(24 + dc * 3 + n // 2)
                            gv = gs.rearrange("p (k a j) -> p k a j", a=2, j=128)
                            gk, gb_, _ = bank()
                            mm([(gk, gv[:, k, n % 2, :], hT[:, k, cols(tt)]) for k in range(KC)],
                               [gsb] + [hb[k][tt] for k in range(KC)], [gb_])
                            t_ap, t_b = at_r.get()
                            act(t_ap, gk, AF.Tanh, [gb_], [t_b], scale=0.5)
                            ok, ob_, _ = bank()
                            mm([(ok, wbv[:, n, k, :], zT[:, n * 4 + k, cols(tt)]) for k in range(4)],
                               [wbb] + [zb[n * 4 + k][tt] for k in range(4)], [ob_])
                            if n == 0:
                                stt(acc_ap, t_ap, 1.0, ok, ALU.add, ALU.mult, [t_b, ob_], [acc_b])
                            else:
                                p_ap, p_b = p_r.get()
                                stt(p_ap, t_ap, 1.0, ok, ALU.add, ALU.mult, [t_b, ob_], [p_b])
                                if n < 3:
                                    tt_(acc_ap, acc_ap, p_ap, ALU.add, [acc_b, p_b], [acc_b])
                                else:
                                    tt_(mT[:, dc, cols(tt)], acc_ap, p_ap, ALU.add, [acc_b, p_b], [mb[dc][tt]])
                    release(24 + dc * 3 + 2)
                wos = [slab(48 + ecp) for ecp in range(4)]
                last_layer = (li + 1 == len(layers))

                def d_groups(tt, ecs):
                    for ec in ecs:
                        ws_, wsb_ = wos[ec // 2]
                        wv = ws_.rearrange("p (k j) -> p k j", j=256)
                        e2 = ec % 2
                        hook()
                        bk, bb_, _ = bank()
                        mm([(bk, wv[:, k, e2 * 128:(e2 + 1) * 128], mT[:, k, cols(tt)]) for k in range(KC)],
                           [wsb_] + [mb[k][tt] for k in range(KC)], [bb_])
                        stt(xT[:, ec, cols(tt)], bk, 0.5, xT[:, ec, cols(tt)], ALU.mult, ALU.add,
                            [bb_, xb[ec][tt]], [xb[ec][tt]])

                def finish_tile(tt):
                    ov = out_d.rearrange("(k p) t -> p k t", p=128)[:, :, u * T + tt * TS:u * T + (tt + 1) * TS]
                    if final:
                        r_ap, r_b = rms_stats(tt)
                        obufs = []
                        for k in range(KC):
                            ob = [zb[2 * k + tt][0], zb[2 * k + tt][1]]
                            obufs += ob
                            stt(outv[:, k, cols(tt)], xT[:, k, cols(tt)], pp[:, PP_FINAL + k:PP_FINAL + k + 1], r_ap,
                                ALU.mult, ALU.mult, [xb[k][tt], r_b] + constb, ob)
                        tok = dma("pool", ov, outv[:, :, cols(tt)], obufs, [], "os%d" % tt)
                    else:
                        tok = dma("pool", ov, xT[:, :, cols(tt)], [xb[k][tt] for k in range(KC)], [], "os%d" % tt)
                    out_toks.append(tok)
                    if u + 1 < n_units:
                        xload(u + 1, tt)

                d_groups(0, range(KC))
                d_groups(1, range(0, 4))
                if not last_layer:
                    phase_a(layers[li + 1], 0)
                else:
                    finish_tile(0)
                d_groups(1, range(4, KC))
                release(51)
                if not last_layer:
                    phase_a(layers[li + 1], 1)
                else:
                    finish_tile(1)
                    if u + 1 < n_units:
                        phase_a(layers[0], 0)
                        phase_a(layers[0], 1)
        S.final_wait("pool", out_toks[-2:])

        with nc.Block() as block:
            def run(name):
                def body(e):
                    for waits, fn, inc_key, inc_amt in S.q[name]:
                        for k, c in waits:
                            e.wait_ge(sems[k], c)
                        if fn is not None:
                            ins = fn(e)
                            if inc_key is not None:
                                ins.then_inc(sems[inc_key], inc_amt)
                return body
            block.tensor(run("pe"))
            block.scalar(run("act"))
            block.vector(run("dve"))
            block.gpsimd(run("pool"))
            block.sync(run("sp"))
    return nc


def _host_consts():
    ident = np.eye(128, dtype=np.float32)
    s = np.arange(128)[:, None]
    t = np.arange(128)[None, :]
    mask = (t >= s).astype(np.float32)
    coef = np.zeros((4, 16), np.float32)
    for g in range(4):
        win = 2 ** (g + 1)
        for tt in range(16):
            if tt < win - 1:
                coef[g, tt] = 1.0 / (tt + 1) - 1.0 / win
    coefb = np.broadcast_to(coef.reshape(1, 64), (128, 64))
    return np.ascontiguousarray(np.concatenate([ident, mask, coefb], axis=1), dtype=np.float32)


def _host_params(norm_g, pool_scale, conv_w, conv_b, conv_ln_g, conv_ln_b, sc_w, final_g):
    pp = np.zeros((128, NPP), np.float32)
    for l in range(2):
        o = l * PP_L
        pp[:, o + PP_NORMG:o + PP_NORMG + 8] = norm_g[l].reshape(8, 128).T
        pp[:, o + PP_PSCALE:o + PP_PSCALE + 4] = pool_scale[l].reshape(4, 128).T
        pp[:, o + PP_CONVB:o + PP_CONVB + 4] = conv_b[l].reshape(4, 128).T
        pp[:, o + PP_CLNG:o + PP_CLNG + 4] = conv_ln_g[l].reshape(4, 128).T
        pp[:, o + PP_CLNB:o + PP_CLNB + 4] = conv_ln_b[l].reshape(4, 128).T
        cw = conv_w[l].reshape(CONV_K, 4, 128)
        pp[:, o + PP_CONVW:o + PP_CONVW + 124] = cw.transpose(2, 1, 0).reshape(128, 124)
        sw = sc_w[l].reshape(3, 4, 128)
        pp[:, o + PP_SCW:o + PP_SCW + 12] = sw.transpose(2, 1, 0).reshape(128, 12)
    pp[:, PP_FINAL:PP_FINAL + 8] = final_g.reshape(8, 128).T
    return pp


_NC_CACHE = {}


def _get_nc(layers, final, n_units):
    key = (tuple(layers), final, n_units)
    if key not in _NC_CACHE:
        _NC_CACHE[key] = _build(list(layers), final, n_units)
    return _NC_CACHE[key]


def _common_maps(w_in, w_branch, w_o, norm_g, pool_w, pool_scale, conv_w, conv_b, conv_ln_g, conv_ln_b,
                 sgu_ln_g, sgu_ln_b, sgu_w, sgu_b, sc_w, final_g):
    f = lambda a: np.ascontiguousarray(np.asarray(a), dtype=np.float32)
    pp = _host_params(f(norm_g), f(pool_scale), f(conv_w), f(conv_b), f(conv_ln_g), f(conv_ln_b), f(sc_w), f(final_g))
    bcv = np.stack([np.stack([f(sgu_ln_g)[l], f(sgu_ln_b)[l]]) for l in range(2)]).reshape(1, 4 * 512)
    bcv = np.ascontiguousarray(np.broadcast_to(bcv, (128, 4 * 512)))
    poolw = np.ascontiguousarray(f(pool_w).reshape(8, 128, 128).transpose(1, 0, 2).reshape(128, 8 * 128))
    sguw = np.ascontiguousarray(f(sgu_w).reshape(8, 128, 128).transpose(2, 0, 1).reshape(128, 8 * 128))
    sgub = np.ascontiguousarray(f(sgu_b).reshape(1, 1024))
    return {"w_in": f(w_in), "w_branch": f(w_branch), "w_o": f(w_o), "pp": pp, "bc": bcv,
            "poolw": poolw, "sguw": sguw, "sgub": sgub, "cst": _host_consts()}


def _run(xT_cores, common, layers, final, n_units):
    nc = _get_nc(layers, final, n_units)
    in_maps = []
    for i in range(NCORE):
        m = dict(common)
        m["xT"] = xT_cores[i]
        in_maps.append(m)
    res = run_bass_kernel_spmd(nc, in_maps, core_ids=list(range(NCORE)))
    if DEBUG:
        DBG_OUT["z"] = [np.asarray(r["dbgz"]) for r in res.results]
        DBG_OUT["h"] = [np.asarray(r["dbgh"]) for r in res.results]
        for k in ("dbg1", "dbg2", "dbg3", "dbg4"):
            DBG_OUT[k] = [np.asarray(r[k]) for r in res.results]
    return [np.asarray(r["outT"]) for r in res.results]


FUSED = True


def kernel(x, norm_g, w_in, pool_w, pool_scale, conv_w, conv_b, conv_ln_g, conv_ln_b,
           sgu_ln_g, sgu_ln_b, sgu_w, sgu_b, sc_w, w_branch, w_o, final_g):
    x = np.asarray(x, dtype=np.float32)
    B, S_, Dm = x.shape
    per = B // NCORE
    n_units = per * S_ // T
    common = _common_maps(w_in, w_branch, w_o, norm_g, pool_w, pool_scale, conv_w, conv_b, conv_ln_g, conv_ln_b,
                          sgu_ln_g, sgu_ln_b, sgu_w, sgu_b, sc_w, final_g)
    xT_cores = [np.ascontiguousarray(x[i * per:(i + 1) * per].reshape(per * S_, Dm).T) for i in range(NCORE)]
    if FUSED:
        outs = _run(xT_cores, common, (0, 1), True, n_units)
    else:
        mid = _run(xT_cores, common, (0,), False, n_units)
        outs = _run(mid, common, (1,), True, n_units)
    out = np.empty((B, S_, Dm), np.float32)
    for i in range(NCORE):
        out[i * per:(i + 1) * per] = outs[i].T.reshape(per, S_, Dm)
    return out
```

```python
import numpy as np
from contextlib import ExitStack
import concourse.bass as bass
import concourse.mybir as mybir
from concourse.bass_utils import run_bass_kernel_spmd

F32 = mybir.dt.float32
BF16 = mybir.dt.bfloat16
AF = mybir.ActivationFunctionType
ALU = mybir.AluOpType

D = 1024
KC = 8
T = 1024
NT = 2
TS = 512
NCORE = 8
SEQ = 2048
NSLAB = 52
RING = 6
CONV_K = 31
HALO = 32
RMS_EPS = 1e-6
LN_EPS = 1e-5
PP_NORMG, PP_PSCALE, PP_CONVB, PP_CLNG, PP_CLNB, PP_CONVW, PP_SCW = 0, 8, 12, 16, 20, 24, 148
PP_L = 160
PP_FINAL = 2 * PP_L
NPP = 2 * PP_L + 8
PIECE_ORDER = [0, 1, 2, 3, 4, 6, 7, 5, 9, 10, 8, 11]
PIECE_POS = {p: i for i, p in enumerate(PIECE_ORDER)}
DEBUG = False
DBG_OUT = {}


class Buf:
    __slots__ = ("w", "r", "name")

    def __init__(self, name=""):
        self.w = None
        self.r = []
        self.name = name


class Sched:
    ENGS = ("pe", "act", "dve", "pool", "sp")

    def __init__(self):
        self.q = {e: [] for e in self.ENGS}
        self.cnt = {}
        self.seen = {e: {} for e in self.ENGS}
        self.keys = set(self.ENGS)

    def _emit(self, eng, deps, fn, inc_key, inc_amt):
        need = {}
        for tok in deps:
            if tok is None:
                continue
            k, c, clk = tok
            if eng == "pe" and k == "pe":
                continue
            if need.get(k, (0, None))[0] < c:
                need[k] = (c, clk)
        waits = []
        seen = self.seen[eng]
        for k, (c, clk) in need.items():
            if seen.get(k, 0) >= c:
                continue
            waits.append((k, c))
            seen[k] = c
            if clk:
                for k2, c2 in clk.items():
                    if seen.get(k2, 0) < c2:
                        seen[k2] = c2
        tok = None
        if inc_key is not None:
            self.keys.add(inc_key)
            self.cnt[inc_key] = self.cnt.get(inc_key, 0) + inc_amt
            clk = dict(seen)
            if inc_key == eng:
                clk[eng] = self.cnt[inc_key] - 1
            tok = (inc_key, self.cnt[inc_key], clk)
        self.q[eng].append((waits, fn, inc_key, inc_amt))
        return tok

    def op(self, eng, fn, reads=(), writes=(), dma_key=None):
        deps = []
        for b in reads:
            deps.append(b.w)
        for b in writes:
            deps.append(b.w)
            deps.extend(b.r)
        if dma_key is None:
            tok = self._emit(eng, deps, fn, eng, 1)
        else:
            tok = self._emit(eng, deps, fn, dma_key, 16)
        for b in reads:
            b.r.append(tok)
        for b in writes:
            b.w = tok
            b.r = []
        return tok

    def barrier(self):
        toks = []
        for k in list(self.keys):
            c = self.cnt.get(k, 0)
            if c > 0:
                toks.append((k, c, None))
        for e in ("pe", "act", "dve", "pool", "sp"):
            self._emit(e, toks, None, None, 0)

    def final_wait(self, eng, toks):
        self._emit(eng, toks, None, None, 0)


def _build(layers, final, n_units):
    nc = bass.Bass("TRN2", target_bir_lowering=False)
    NTOK = n_units * T
    xT_d = nc.dram_tensor("xT", [D, NTOK], F32, kind="ExternalInput").ap()
    out_d = nc.dram_tensor("outT", [D, NTOK], F32, kind="ExternalOutput").ap()
    w_in_d = nc.dram_tensor("w_in", [2, D, 10240], F32, kind="ExternalInput").ap()
    w_br_d = nc.dram_tensor("w_branch", [2, 4, 512, 1024], F32, kind="ExternalInput").ap()
    w_o_d = nc.dram_tensor("w_o", [2, 1024, 1024], F32, kind="ExternalInput").ap()
    pp_d = nc.dram_tensor("pp", [128, NPP], F32, kind="ExternalInput").ap()
    bc_d = nc.dram_tensor("bc", [128, 4 * 512], F32, kind="ExternalInput").ap()
    poolw_d = nc.dram_tensor("poolw", [128, 8 * 128], F32, kind="ExternalInput").ap()
    sguw_d = nc.dram_tensor("sguw", [128, 8 * 128], F32, kind="ExternalInput").ap()
    sgub_d = nc.dram_tensor("sgub", [1, 1024], F32, kind="ExternalInput").ap()
    cst_d = nc.dram_tensor("cst", [128, 128 + 128 + 64], F32, kind="ExternalInput").ap()
    wsc_d = nc.dram_tensor("wscratch", [2, NSLAB, 128, 2048], BF16, kind="Internal").ap()
    if DEBUG:
        dbgz_d = nc.dram_tensor("dbgz", [128, 16 * T], BF16, kind="ExternalOutput").ap()
        dbgh_d = nc.dram_tensor("dbgh", [128, 8 * T], BF16, kind="ExternalOutput").ap()
        dbg1_d = nc.dram_tensor("dbg1", [128, 8], F32, kind="ExternalOutput").ap()
        dbg2_d = nc.dram_tensor("dbg2", [128, 16], F32, kind="ExternalOutput").ap()
        dbg3_d = nc.dram_tensor("dbg3", [128, 64], BF16, kind="ExternalOutput").ap()
        dbg4_d = nc.dram_tensor("dbg4", [128, 64], F32, kind="ExternalOutput").ap()

    S = Sched()
    es = ExitStack()

    def sb(name, shape, dt):
        return es.enter_context(nc.sbuf_tensor("sb_" + name, shape, dt))

    with es:
        xT = sb("xT", [128, KC, T], F32)
        hT = sb("hT", [128, KC, T], BF16)
        zT = sb("zT", [128, 16, T], BF16)
        RG = sb("RG", [128, 17792], BF16)
        BB = sb("BB", [128, 4, HALO + T], BF16)
        HS = sb("HS", [128, 2 * 3 * 4, HALO], BF16)
        ring = sb("ring", [128, RING, 2048], BF16)
        pp = sb("pp", [128, NPP], F32)
        wh = sb("wh", [128, 2, 136], F32)
        bc = sb("bc", [128, 4, 512], F32)
        PW = sb("PW", [128, 8, 3, 128], BF16)
        wsT = sb("wsT", [128, 8, 128], BF16)
        SBm = sb("SBm", [128, 1024], BF16)
        E01 = sb("E01", [128, 128], BF16)
        cst = sb("cst", [128, 320], F32)
        ident = sb("ident", [128, 128], BF16)
        onesA = sb("onesA", [128, 128], BF16)
        onesC = sb("onesC", [128, 128], BF16)
        ES = sb("ES", [128, 2, 32], F32)
        cm05 = sb("cm05", [128, 8], F32)
        corr = sb("corr", [128, 4, 16], BF16)
        sq = sb("sq", [128, 2, TS], BF16)
        st = sb("st", [128, 2, TS], F32)
        stgS = sb("stgS", [128, 2, 2048], BF16)
        at = sb("at", [128, 2, TS], F32)
        dt_ = sb("dt", [128, 2, TS], F32)
        yc = sb("yc", [128, 4, TS], F32)
        poolw_f = yc[:, 0:2, :].rearrange("p a (b n) -> p (a b) n", n=128)
        sguw_f = yc[:, 2:4, :].rearrange("p a (b n) -> p (a b) n", n=128)
        sgub_f = at[0:1, 0:2, :].rearrange("p a n -> p (a n)")
        sgub_hf = st[0:1, 0:2, :].rearrange("p a n -> p (a n)")
        sgub_hi = sq[0:1, 0:2, :].rearrange("p a n -> p (a n)")
        vn = sb("vn", [128, 2, TS], F32)
        vg = sb("vg", [128, 2, TS], BF16)
        sgub_lo = vg[0:1, 0:2, :].rearrange("p a n -> p (a n)")
        bnst = sb("bnst", [128, 2, 8], F32)
        bnmv = sb("bnmv", [128, 2, 4], F32)
        ps = es.enter_context(nc.psum_tensor("ps", [128, 8, TS], F32))

        sem_names = list(Sched.ENGS) + ["cst%d" % i for i in range(8)] + \
            ["slot%d" % i for i in range(RING)] + ["misc", "os0", "os1"] + ["xl%d%d" % (a_, b_) for a_ in range(2) for b_ in range(4)] + ["cv%d" % i for i in range(10)] + ["cs%d" % i for i in range(10)]
        sems = {k: es.enter_context(nc.semaphore("s_" + k)) for k in sem_names}

        xb = [[Buf() for _ in range(NT)] for _ in range(KC)]
        hb = [[Buf() for _ in range(NT)] for _ in range(KC)]
        zb = [[Buf() for _ in range(NT)] for _ in range(16)]
        mb = [[Buf() for _ in range(NT)] for _ in range(KC)]
        bbb = [[Buf() for _ in range(NT)] for _ in range(4)]
        bbp = [Buf() for _ in range(4)]
        hsb = [Buf() for _ in range(6)]
        diagb = [Buf() for _ in range(4)]
        scdb = Buf()
        slotb = [Buf() for _ in range(RING)]
        bankb = [Buf() for _ in range(8)]
        cb = Buf("consts")
        esb = [Buf(), Buf()]
        corrb = [Buf() for _ in range(4)]

        class Ring:
            def __init__(self, t, n):
                self.t, self.n, self.i = t, n, 0
                self.b = [Buf() for _ in range(n)]

            def get(self):
                i = self.i
                self.i = (i + 1) % self.n
                return self.t[:, i, :], self.b[i]

        sq_r, st_r, at_r, dt_r = Ring(sq, 2), Ring(st, 2), Ring(at, 2), Ring(dt_, 2)
        vn_r, vg_r = Ring(vn, 2), Ring(vg, 2)
        ycb_r, ysq_r = vg_r, sq_r
        ycbuf = [Buf() for _ in range(4)]
        bn_r = [Buf(), Buf()]
        bn_i = [0]

        diag = RG[:, 0:124 * 128].rearrange("p (j m) -> p j m", m=128)
        scd = RG[:, 15872:15872 + 12 * 128].rearrange("p (j m) -> p j m", m=128)
        mT = RG[:, 0:8192].rearrange("p (k t) -> p k t", t=T)
        accp = RG[:, 8192:8192 + 4096].bitcast(F32).rearrange("p (k t) -> p k t", t=TS)
        acc_r = Ring(accp[:, 0:2, :], 2)
        p_r = Ring(accp[:, 2:4, :], 2)

        held = set()
        bank_i = [0]

        def bank(hold=False):
            while True:
                i = bank_i[0]
                bank_i[0] = (i + 1) % 8
                if i not in held:
                    break
            if hold:
                held.add(i)
            return ps[:, i, :], bankb[i], i

        def mm(mms, reads, writes, flags=None):
            n = len(mms)

            def fn(e):
                ins = None
                for i, (o, l, r) in enumerate(mms):
                    if flags is None:
                        s0, s1 = (i == 0), (i == n - 1)
                    else:
                        s0, s1 = flags[i]
                    ins = e.matmul(o, l, r, start=s0, stop=s1)
                return ins
            return S.op("pe", fn, reads, writes)

        def act(out, in_, func, reads, writes, scale=1.0, bias=0.0):
            return S.op("act", lambda e: e.activation(out, in_, func, bias=bias, scale=scale), reads, writes)

        def dve(fn, reads, writes):
            return S.op("dve", fn, reads, writes)

        def tt_(out, a, b, op, reads, writes, eng="dve"):
            return S.op(eng, lambda e: e.tensor_tensor(out, a, b, op), reads, writes)

        def stt(out, in0, scalar, in1, op0, op1, reads, writes, eng="dve"):
            return S.op(eng, lambda e: e.scalar_tensor_tensor(out, in0, scalar, in1, op0, op1), reads, writes)

        def ts_(out, in0, s1, s2, op0, op1, reads, writes, eng="dve"):
            if s2 is None:
                return S.op(eng, lambda e: e.tensor_scalar(out, in0, s1, None, op0), reads, writes)
            return S.op(eng, lambda e: e.tensor_scalar(out, in0, s1, s2, op0, op1), reads, writes)

        def dma(eng, out, in_, reads, writes, key):
            return S.op(eng, lambda e: e.dma_start(out=out, in_=in_), reads, writes, dma_key=key)

        dma("sp", pp[:, :], pp_d, [], [cb], "cst0")
        dma("sp", bc[:, :, :], bc_d.rearrange("p (a n) -> p a n", n=512), [], [cb], "cst1")
        dma("sp", poolw_f, poolw_d.rearrange("p (a n) -> p a n", n=128), [], [cb], "cst2")
        dma("sp", sguw_f, sguw_d.rearrange("p (a n) -> p a n", n=128), [], [cb], "cst3")
        dma("sp", sgub_f, sgub_d, [], [cb], "cst4")
        dma("sp", cst[:, :], cst_d, [], [cb], "cst5")
        c2 = Buf()
        dve(lambda e: e.tensor_copy(ident[:, :], cst[:, 0:128]), [cb], [c2])
        dve(lambda e: e.memset(onesA[:, :], 1.0 / 1024.0), [], [c2])
        dve(lambda e: e.memset(onesC[:, :], 1.0 / 512.0), [], [c2])
        dve(lambda e: e.memset(E01[:, :], 0.0), [], [c2])
        dve(lambda e: e.memset(cm05[:, :], -0.5), [], [c2])
        dve(lambda e: e.memset(SBm[:, :], 0.0), [], [c2])
        dve(lambda e: e.memset(ES[:, :, :], 0.0), [], [esb[0], esb[1]])
        dve(lambda e: e.memset(BB[:, :, :], 0.0), [], [bbp[c] for c in range(4)])
        c3 = Buf()
        dve(lambda e: e.memset(E01[0:2, :], 1.0), [c2], [c3])
        for l in range(2):
            ts_(wh[:, l, 0:124], pp[:, l * PP_L + PP_CONVW: l * PP_L + PP_CONVW + 124], 0.5, None, ALU.mult, None, [cb], [c3])
            dve(lambda e, l=l: e.tensor_copy(wh[:, l, 124:136], pp[:, l * PP_L + PP_SCW: l * PP_L + PP_SCW + 12]), [cb], [c3])
            for g in range(4):
                win = float(2 ** (g + 1))
                j = l * 4 + g
                dve(lambda e, j=j: e.tensor_copy(PW[:, j, 0, :], poolw_f[:, j, :]), [cb], [c3])
                ts_(PW[:, j, 1, :], poolw_f[:, j, :], 1.0 / win - 1.0, None, ALU.mult, None, [cb], [c3])
                ts_(PW[:, j, 2, :], poolw_f[:, j, :], 1.0 / win, None, ALU.mult, None, [cb], [c3])
                tt_(wsT[:, j, :], sguw_f[:, j, :], cst[:, 128:256], ALU.mult, [cb], [c3])
        c4 = Buf()
        dve(lambda e: e.tensor_copy(sgub_hi, sgub_f), [cb], [c4])
        c5 = Buf()
        dve(lambda e: e.tensor_copy(sgub_hf, sgub_hi), [c4], [c5])
        c6 = Buf()
        tt_(sgub_lo, sgub_f, sgub_hf, ALU.subtract, [cb, c5], [c6])
        dma("sp", SBm[0:1, :], sgub_hi, [c4, c2], [c3], "cst6")
        dma("sp", SBm[1:2, :], sgub_lo, [c6, c2], [c3], "cst7")
        constb = [cb, c2, c3]

        wscbufs = [[Buf() for _ in range(NSLAB)] for _ in range(2)]

        def convert_slab(l, s, stg_ap, stg_buf, kc_, ks_, store_eng):
            win_v = w_in_d[l].rearrange("(k p) n -> p k n", p=128)
            if s < 24:
                piece, half = PIECE_ORDER[s // 2], s % 2
                c0 = piece * 512 + half * 256
                parts = [(stg_ap.rearrange("p (k j) -> p k j", j=256), win_v[:, :, c0:c0 + 256])]
            elif s < 48:
                dc, r = (s - 24) // 3, (s - 24) % 3
                parts = []
                if r < 2:
                    dv = stg_ap.rearrange("p (k a j) -> p k a j", a=2, j=128)
                    for a_ in range(2):
                        n = 2 * r + a_
                        c0 = 6144 + n * 1024 + dc * 128
                        parts.append((dv[:, :, a_, :], win_v[:, :, c0:c0 + 128]))
                else:
                    dv = stg_ap.rearrange("p (a k j) -> p a k j", a=4, j=128)
                    for n in range(4):
                        src = w_br_d[l, n].rearrange("(k p) d -> p k d", p=128)
                        parts.append((dv[:, n, :, :], src[:, :, dc * 128:(dc + 1) * 128]))
            else:
                ecp = s - 48
                src = w_o_d[l].rearrange("(k p) e -> p k e", p=128)
                parts = [(stg_ap.rearrange("p (k j) -> p k j", j=256), src[:, :, ecp * 256:(ecp + 1) * 256])]
            for d_, s_ in parts:
                dma("pool", d_, s_, [], [stg_buf], kc_)
            if store_eng is not None:
                dma(store_eng, wsc_d[l, s], stg_ap, [stg_buf], [wscbufs[l][s]], ks_)

        NB = 3
        pstg = [zT[:, 2 * j:2 * j + 2, :].rearrange("p a t -> p (a t)") for j in range(NB)]
        pstgb = [Buf() for _ in range(NB)]
        NPRO = 12
        for s_i in range(NPRO):
            j = s_i % NB
            convert_slab(layers[0], s_i, pstg[j], pstgb[j], "cv%d" % j, "cs%d" % j, "sp")
        S.barrier()

        bgq = [(layers[0], s_i) for s_i in range(NPRO, NSLAB)] + [(l, s_i) for l in layers[1:] for s_i in range(NSLAB)]
        unconverted = set(bgq)
        sstg = [stgS[:, 0, :], stgS[:, 1, :]]
        sstgb = [Buf(), Buf()]
        bg = {"n": 0, "calls": 0}

        def hook(force=False):
            bg["calls"] += 1
            if not bgq and not bg.get("pend"):
                return
            if not force and bg["calls"] % 4 != 0:
                return
            pend = bg.get("pend")
            if pend is not None:
                l_, s_, j = pend
                dma("sp", wsc_d[l_, s_], sstg[j], [sstgb[j]], [wscbufs[l_][s_]], "cs%d" % (8 + j))
                unconverted.discard((l_, s_))
                bg["pend"] = None
            if bgq:
                l_, s_ = bgq.pop(0)
                j = bg["n"] % 2
                bg["n"] += 1
                convert_slab(l_, s_, sstg[j], sstgb[j], "cv%d" % (8 + j), None, None)
                bg["pend"] = (l_, s_, j)

        seq = []
        for u in range(n_units):
            for l in layers:
                for s in range(NSLAB):
                    seq.append((l, s))
        st8 = {"loaded": 0, "released": 0, "base": 0}

        def slab_pump():
            while st8["loaded"] < len(seq) and st8["loaded"] < st8["released"] + RING:
                j = st8["loaded"]
                l, s = seq[j]
                slot = j % RING
                while (l, s) in unconverted:
                    hook(force=True)
                dma("sp", ring[:, slot, :], wsc_d[l, s], [wscbufs[l][s]], [slotb[slot]], "slot%d" % slot)
                st8["loaded"] += 1

        def slab(s):
            j = st8["base"] + s
            assert j < st8["loaded"], (j, st8)
            slot = j % RING
            return ring[:, slot, :], slotb[slot]

        def release(upto_s):
            r = st8["base"] + upto_s + 1
            if r > st8["released"]:
                st8["released"] = r
            slab_pump()

        slab_pump()

        def cols(tt):
            return slice(tt * TS, (tt + 1) * TS)

        def proj(s, sub, tt):
            hook()
            w, wbuf = slab(s)
            wv = w.rearrange("p (k j) -> p k j", j=256)
            bk, bb_, _ = bank()
            mm([(bk, wv[:, k, sub * 128:(sub + 1) * 128], hT[:, k, cols(tt)]) for k in range(KC)],
               [wbuf] + [hb[k][tt] for k in range(KC)], [bb_])
            return bk, bb_

        def pcol(l, base, c):
            o = l * PP_L + base + c
            return pp[:, o:o + 1]

        def rms_stats(tt):
            bk, bb_, _ = bank()
            for k in range(KC):
                s_ap, s_b = sq_r.get()
                act(s_ap, xT[:, k, cols(tt)], AF.Square, [xb[k][tt]], [s_b])
                mm([(bk, onesA[:, :], s_ap)], [s_b] + constb, [bb_], flags=[(k == 0, k == KC - 1)])
            r_ap, r_b = st_r.get()
            ts_(r_ap, bk, RMS_EPS, None, ALU.add, None, [bb_], [r_b])
            act(r_ap, r_ap, AF.Ln, [r_b], [r_b])
            act(r_ap, r_ap, AF.Exp, [r_b], [r_b], scale=-0.5)
            return r_ap, r_b

        def xload(u_, tt_i):
            xv = xT_d.rearrange("(k p) t -> p k t", p=128)[:, :, u_ * T + tt_i * TS:u_ * T + (tt_i + 1) * TS]
            for k0 in range(0, KC, 2):
                dma("pool", xT[:, k0:k0 + 2, cols(tt_i)], xv[:, k0:k0 + 2, :], [],
                    [xb[k][tt_i] for k in range(k0, k0 + 2)], "xl%d%d" % (tt_i, k0 // 2))

        def phase_a(l_, tt_i):
            r_ap, r_b = rms_stats(tt_i)
            for k in range(KC):
                stt(hT[:, k, cols(tt_i)], xT[:, k, cols(tt_i)], pcol(l_, PP_NORMG, k), r_ap, ALU.mult, ALU.mult,
                    [xb[k][tt_i], r_b] + constb, [hb[k][tt_i]])

        outv = zT[:, :, :].rearrange("p a t -> p (a t)").bitcast(F32).rearrange("p (k t) -> p k t", t=T)
        fence = sb("fence", [128, 8], F32)
        fenceb = Buf()
        rg_cd_bufs = [mb[k][tt] for k in range(KC) for tt in range(NT)] + acc_r.b + p_r.b

        out_toks = []
        for u in range(n_units):
            seq_start = (u % 2 == 0)
            if u == 0:
                for tt in range(NT):
                    xload(0, tt)
                    phase_a(layers[0], tt)
            for li, l in enumerate(layers):
                st8["base"] = (u * len(layers) + li) * NSLAB

                def load_halo(br, l=l, seq_start=seq_start):
                    for c in range(4):
                        d_ = BB[:, c, 0:HALO]
                        if seq_start:
                            S.op("pool", lambda e, d_=d_: e.memset(d_, 0.0), [], [bbp[c]])
                        else:
                            s_ = HS[:, (l * 3 + br) * 4 + c, :]
                            S.op("pool", lambda e, d_=d_, s_=s_: e.tensor_copy(d_, s_), [hsb[l * 3 + br]], [bbp[c]])

                def save_halo(br, l=l):
                    for c in range(4):
                        d_ = HS[:, (l * 3 + br) * 4 + c, :]
                        s_ = BB[:, c, T:T + HALO]
                        S.op("pool", lambda e, d_=d_, s_=s_: e.tensor_copy(d_, s_), [bbb[c][NT - 1]], [hsb[l * 3 + br]])

                load_halo(0)
                dq = []
                for c in range(4):
                    for k0, k1 in ((0, 8), (8, 16), (16, 24), (24, CONV_K)):
                        o_ = diag[:, c * CONV_K + k0:c * CONV_K + k1, :]
                        i0 = ident[:, :].unsqueeze(1).broadcast_to([128, k1 - k0, 128])
                        i1 = wh[:, l, c * CONV_K + k0:c * CONV_K + k1].unsqueeze(2).broadcast_to([128, k1 - k0, 128])
                        dq.append((o_, i0, i1, [diagb[c]] + (rg_cd_bufs if k0 == 0 else [])))
                o_ = scd[:, 0:12, :]
                i0 = ident[:, :].unsqueeze(1).broadcast_to([128, 12, 128])
                i1 = wh[:, l, 124:136].unsqueeze(2).broadcast_to([128, 12, 128])
                dq.append((o_, i0, i1, [scdb] + rg_cd_bufs))

                def diag_some(n):
                    for _ in range(n):
                        if dq:
                            o_, i0, i1, wr = dq.pop(0)
                            tt_(o_, i0, i1, ALU.mult, constb, wr)

                for tt in range(NT):
                    for c in range(4):
                        bk, bb_ = proj(c // 2, c % 2, tt)
                        act(BB[:, c, HALO + tt * TS:HALO + (tt + 1) * TS], bk, AF.Copy, [bb_], [bbb[c][tt]])
                        if seq_start and tt == 0:
                            win = 2 ** (c + 1)
                            cur = 0
                            tt_(ES[:, 0, 16:32], BB[:, c, HALO:HALO + 16], BB[:, c, HALO - 1:HALO + 15], ALU.add,
                                [bbb[c][0], bbp[c]], [esb[0]])
                            sh = 2
                            while sh < win:
                                tt_(ES[:, 1 - cur, 16:32], ES[:, cur, 16:32], ES[:, cur, 16 - sh:32 - sh], ALU.add,
                                    [esb[cur]], [esb[1 - cur]])
                                cur = 1 - cur
                                sh *= 2
                            tt_(corr[:, c, :], ES[:, cur, 16:32], cst[:, 256 + c * 16:256 + (c + 1) * 16], ALU.mult,
                                [esb[cur]] + constb, [corrb[c]])
                release(1)
                for tt in range(NT):
                    for c in range(4):
                        win = 2 ** (c + 1)
                        j = l * 4 + c
                        rd = [bbb[c][tt], bbp[c] if tt == 0 else bbb[c][tt - 1]] + constb
                        mms = []
                        for sft in range(win):
                            o = HALO + tt * TS - sft
                            mms.append((None, PW[:, j, 1 if sft == 0 else 2, :], BB[:, c, o:o + TS]))
                        bk, bb_, _ = bank()
                        mms = [(bk, a, b) for (_, a, b) in mms]
                        flags = [(i == 0, i == len(mms) - 1) for i in range(len(mms))]
                        if seq_start and tt == 0:
                            mms.append((bk[:, 0:16], PW[:, j, 0, :], corr[:, c, :]))
                            flags = [(i == 0, False) for i in range(len(mms) - 1)] + [(False, True)]
                            rd = rd + [corrb[c]]
                        mm(mms, rd, [bb_], flags=flags)
                        gk, gb_ = proj(2 + c // 2, c % 2, tt)
                        a_ap, a_b = at_r.get()
                        act(a_ap, gk, AF.Silu, [gb_], [a_b])
                        stt(zT[:, 0 * 4 + c, cols(tt)], bk, pcol(l, PP_PSCALE, c), a_ap, ALU.mult, ALU.mult,
                            [bb_, a_b] + constb, [zb[c][tt]])
                        diag_some(1)
                release(3)
                save_halo(0)

                sgu_gens = [None]

                def sgu_tile(tt, l=l):
                    Sk = [bank(hold=True) for _ in range(4)]

                    def small(tb, v_ap, v_b):
                        for g in range(4):
                            o = Sk[g][0][:, tb * 128:(tb + 1) * 128]
                            j = l * 4 + g
                            mm([(o, v_ap[:, g * 128:(g + 1) * 128], wsT[:, j, :]),
                                (o, E01[:, :], SBm[:, j * 128:(j + 1) * 128])],
                               [v_b] + constb, [Sk[g][1]])

                    def vproj(tb):
                        w0, wb0 = slab(10)
                        w1, wb1 = slab(11)
                        bk, bb_, _ = bank()
                        t0 = tt * TS + tb * 128
                        mms, flags = [], []
                        for hh, w in enumerate((w0, w1)):
                            wv = w.rearrange("p (k j) -> p k j", j=256)
                            for k in range(KC):
                                mms.append((bk[:, hh * 256:(hh + 1) * 256], hT[:, k, t0:t0 + 128], wv[:, k, :]))
                                flags.append((k == 0, k == KC - 1))
                        mm(mms, [wb0, wb1] + [hb[k][tt] for k in range(KC)], [bb_], flags=flags)
                        i = bn_i[0]
                        bn_i[0] = 1 - i
                        dve(lambda e: e.bn_stats(bnst[:, i, 0:6], bk), [bb_], [bn_r[i]])
                        dve(lambda e: e.bn_aggr(bnmv[:, i, 0:2], bnst[:, i, 0:6]), [bn_r[i]], [bn_r[i]])
                        ts_(bnmv[:, i, 2:3], bnmv[:, i, 1:2], LN_EPS, None, ALU.add, None, [bn_r[i]], [bn_r[i]])
                        tt_(bnmv[:, i, 2:3], bnmv[:, i, 2:3], cm05[:, 0:1], ALU.pow, [bn_r[i]] + constb, [bn_r[i]], eng="pool")
                        stt(bnmv[:, i, 3:4], bnmv[:, i, 0:1], -1.0, bnmv[:, i, 2:3], ALU.mult, ALU.mult, [bn_r[i]], [bn_r[i]])
                        n_ap, n_b = vn_r.get()
                        act(n_ap, bk, AF.Identity, [bb_, bn_r[i]], [n_b], scale=bnmv[:, i, 2:3], bias=bnmv[:, i, 3:4])
                        tt_(n_ap, n_ap, bc[:, l * 2 + 0, :], ALU.mult, [n_b] + constb, [n_b])
                        g_ap, g_b = vg_r.get()
                        tt_(g_ap, n_ap, bc[:, l * 2 + 1, :], ALU.add, [n_b] + constb, [g_b])
                        return g_ap, g_b

                    v0 = vproj(0)
                    v1 = vproj(1)
                    yield
                    small(0, *v0)
                    v2 = vproj(2)
                    small(1, *v1)
                    v3 = vproj(3)
                    small(2, *v2)
                    if tt == NT - 1:
                        release(11)
                    for c in range(4):
                        gk, gb_ = proj(12 + c // 2, c % 2, tt)
                        a_ap, a_b = at_r.get()
                        act(a_ap, gk, AF.Silu, [gb_], [a_b])
                        uk, ub_ = proj(14 + c // 2, c % 2, tt)
                        d_ap, d_b = dt_r.get()
                        tt_(d_ap, uk, a_ap, ALU.mult, [ub_, a_b], [d_b])
                        if c == 0:
                            small(3, *v3)
                        tt_(zT[:, 8 + c, cols(tt)], Sk[c][0], d_ap, ALU.mult, [Sk[c][1], d_b], [zb[8 + c][tt]])
                    for g in range(4):
                        held.discard(Sk[g][2])

                load_halo(1)
                for tt in range(NT):
                    for c in range(4):
                        bk, bb_ = proj(6 + c // 2, c % 2, tt)
                        a_ap, a_b = at_r.get()
                        act(a_ap, bk, AF.Tanh, [bb_], [a_b], scale=0.5)
                        ak, ab_ = proj(4 + c // 2, c % 2, tt)
                        stt(BB[:, c, HALO + tt * TS:HALO + (tt + 1) * TS], a_ap, 1.0, ak, ALU.add, ALU.mult,
                            [a_b, ab_], [bbb[c][tt]])
                        diag_some(1)
                diag_some(100)
                release(7)
                for tt in range(NT):
                    mk, mbk, _ = bank()
                    qk, qbk, _ = bank()

                    def stats_mm(c, yb_ap, yb_b, ys_ap, ys_b, mk=mk, mbk=mbk, qk=qk, qbk=qbk):
                        mm([(mk, onesC[:, :], yb_ap)], [yb_b] + constb, [mbk], flags=[(c == 0, c == 3)])
                        mm([(qk, onesC[:, :], ys_ap)], [ys_b] + constb, [qbk], flags=[(c == 0, c == 3)])

                    pend_s = None
                    for c in range(4):
                        bk, bb_, _ = bank()
                        rd = [bbb[c][tt], bbp[c] if tt == 0 else bbb[c][tt - 1], diagb[c]]
                        mms = []
                        for k in range(CONV_K):
                            o = HALO + tt * TS - (CONV_K - 1) + k
                            mms.append((bk, diag[:, c * CONV_K + k, :], BB[:, c, o:o + TS]))
                        mm(mms, rd, [bb_])
                        act(yc[:, c, :], bk, AF.Identity, [bb_] + constb, [ycbuf[c]], bias=pcol(l, PP_CONVB, c))
                        yb_ap, yb_b = ycb_r.get()
                        act(yb_ap, bk, AF.Identity, [bb_] + constb, [yb_b], bias=pcol(l, PP_CONVB, c))
                        ys_ap, ys_b = ysq_r.get()
                        act(ys_ap, bk, AF.Square, [bb_] + constb, [ys_b], bias=pcol(l, PP_CONVB, c))
                        if pend_s is not None:
                            stats_mm(*pend_s)
                        pend_s = (c, yb_ap, yb_b, ys_ap, ys_b)
                    stats_mm(*pend_s)
                    m_ap, m_b = st_r.get()
                    v_ap, v_b = st_r.get()
                    dve(lambda e, m_ap=m_ap, mk=mk: e.tensor_copy(m_ap, mk), [mbk], [m_b])
                    tt_(v_ap, m_ap, m_ap, ALU.mult, [m_b], [v_b])
                    stt(v_ap, qk, LN_EPS, v_ap, ALU.add, ALU.subtract, [qbk, v_b], [v_b])
                    act(v_ap, v_ap, AF.Ln, [v_b], [v_b])
                    act(v_ap, v_ap, AF.Exp, [v_b], [v_b], scale=-0.5)
                    tt_(m_ap, m_ap, v_ap, ALU.mult, [m_b, v_b], [m_b])
                    if tt == NT - 1:
                        sgu_gens[0] = sgu_tile(0)
                        next(sgu_gens[0])

                    def tail_a(c):
                        tt_(yc[:, c, :], yc[:, c, :], v_ap, ALU.mult, [ycbuf[c], v_b], [ycbuf[c]])
                        tt_(yc[:, c, :], yc[:, c, :], m_ap, ALU.subtract, [ycbuf[c], m_b], [ycbuf[c]])
                        s_ap, s_b = dt_r.get()
                        act(s_ap, yc[:, c, :], AF.Silu, [ycbuf[c]] + constb, [s_b],
                            scale=pcol(l, PP_CLNG, c), bias=pcol(l, PP_CLNB, c))
                        return s_ap, s_b

                    def tail_b(c, s_ap, s_b):
                        gk, gb_ = proj(8 + c // 2, c % 2, tt)
                        g_ap, g_b = at_r.get()
                        act(g_ap, gk, AF.Silu, [gb_], [g_b])
                        tt_(zT[:, 4 + c, cols(tt)], s_ap, g_ap, ALU.mult, [s_b, g_b], [zb[4 + c][tt]])

                    pend_c = None
                    for c in range(4):
                        sa = tail_a(c)
                        if pend_c is not None:
                            tail_b(*pend_c)
                        pend_c = (c,) + sa
                    tail_b(*pend_c)
                release(9)
                save_halo(1)

                for tt in range(NT):
                    if tt == 0:
                        g_ = sgu_gens[0]
                    else:
                        g_ = sgu_tile(tt)
                        next(g_)
                    for _ in g_:
                        pass
                release(15)

                load_halo(2)
                for tt in range(NT):
                    for c in range(4):
                        ck, cb_ = proj(16 + c // 2, c % 2, tt)
                        a_ap, a_b = at_r.get()
                        act(a_ap, ck, AF.Copy, [cb_], [a_b])
                        xk, xb_ = proj(18 + c // 2, c % 2, tt)
                        tt_(BB[:, c, HALO + tt * TS:HALO + (tt + 1) * TS], xk, a_ap, ALU.mult, [xb_, a_b], [bbb[c][tt]])
                release(19)
                for tt in range(NT):
                    for c in range(4):
                        bk, bb_, _ = bank()
                        rd = [bbb[c][tt], bbp[c] if tt == 0 else bbb[c][tt - 1], scdb]
                        mms = []
                        for k in range(3):
                            o = HALO + tt * TS - 2 + k
                            mms.append((bk, scd[:, c * 3 + k, :], BB[:, c, o:o + TS]))
                        mm(mms, rd, [bb_])
                        gk, gb_ = proj(22 + c // 2, c % 2, tt)
                        a_ap, a_b = at_r.get()
                        act(a_ap, gk, AF.Silu, [gb_], [a_b])
                        sk, sb_ = proj(20 + c // 2, c % 2, tt)
                        d_ap, d_b = dt_r.get()
                        tt_(d_ap, sk, a_ap, ALU.mult, [sb_, a_b], [d_b])
                        tt_(zT[:, 12 + c, cols(tt)], bk, d_ap, ALU.mult, [bb_, d_b], [zb[12 + c][tt]])
                release(23)
                save_halo(2)
                dve(lambda e: e.memset(fence[:, 0:1], 0.0), [], [fenceb] + diagb + [scdb] + rg_cd_bufs)

                for dc in range(8):
                    wbs, wbb = slab(24 + dc * 3 + 2)
                    wbv = wbs.rearrange("p (a k j) -> p a k j", a=4, j=128)
                    for tt in range(NT):
                        acc_ap, acc_b = acc_r.get()
                        for n in range(4):
                            hook()
                            hook()
                            gs, gsb = slab(24 + dc * 3 + n // 2)
                            gv = gs.rearrange("p (k a j) -> p k a j", a=2, j=128)
                            gk, gb_, _ = bank()
                            mm([(gk, gv[:, k, n % 2, :], hT[:, k, cols(tt)]) for k in range(KC)],
                               [gsb] + [hb[k][tt] for k in range(KC)], [gb_])
                            t_ap, t_b = at_r.get()
                            act(t_ap, gk, AF.Tanh, [gb_], [t_b], scale=0.5)
                            ok, ob_, _ = bank()
                            mm([(ok, wbv[:, n, k, :], zT[:, n * 4 + k, cols(tt)]) for k in range(4)],
                               [wbb] + [zb[n * 4 + k][tt] for k in range(4)], [ob_])
                            if n == 0:
                                stt(acc_ap, t_ap, 1.0, ok, ALU.add, ALU.mult, [t_b, ob_], [acc_b])
                            else:
                                p_ap, p_b = p_r.get()
                                stt(p_ap, t_ap, 1.0, ok, ALU.add, ALU.mult, [t_b, ob_], [p_b])
                                if n < 3:
                                    tt_(acc_ap, acc_ap, p_ap, ALU.add, [acc_b, p_b], [acc_b])
                                else:
                                    tt_(mT[:, dc, cols(tt)], acc_ap, p_ap, ALU.add, [acc_b, p_b], [mb[dc][tt]])
                    release(24 + dc * 3 + 2)
                wos = [slab(48 + ecp) for ecp in range(4)]
                last_layer = (li + 1 == len(layers))

                def d_groups(tt, ecs):
                    for ec in ecs:
                        ws_, wsb_ = wos[ec // 2]
                        wv = ws_.rearrange("p (k j) -> p k j", j=256)
                        e2 = ec % 2
                        hook()
                        bk, bb_, _ = bank()
                        mm([(bk, wv[:, k, e2 * 128:(e2 + 1) * 128], mT[:, k, cols(tt)]) for k in range(KC)],
                           [wsb_] + [mb[k][tt] for k in range(KC)], [bb_])
                        stt(xT[:, ec, cols(tt)], bk, 0.5, xT[:, ec, cols(tt)], ALU.mult, ALU.add,
                            [bb_, xb[ec][tt]], [xb[ec][tt]])

                def finish_tile(tt):
                    ov = out_d.rearrange("(k p) t -> p k t", p=128)[:, :, u * T + tt * TS:u * T + (tt + 1) * TS]
                    if final:
                        r_ap, r_b = rms_stats(tt)
                        obufs = []
                        for k in range(KC):
                            ob = [zb[2 * k + tt][0], zb[2 * k + tt][1]]
                            obufs += ob
                            stt(outv[:, k, cols(tt)], xT[:, k, cols(tt)], pp[:, PP_FINAL + k:PP_FINAL + k + 1], r_ap,
                                ALU.mult, ALU.mult, [xb[k][tt], r_b] + constb, ob)
                        tok = dma("pool", ov, outv[:, :, cols(tt)], obufs, [], "os%d" % tt)
                    else:
                        tok = dma("pool", ov, xT[:, :, cols(tt)], [xb[k][tt] for k in range(KC)], [], "os%d" % tt)
                    out_toks.append(tok)
                    if u + 1 < n_units:
                        xload(u + 1, tt)

                d_groups(0, range(KC))
                d_groups(1, range(0, 4))
                if not last_layer:
                    phase_a(layers[li + 1], 0)
                else:
                    finish_tile(0)
                d_groups(1, range(4, KC))
                release(51)
                if not last_layer:
                    phase_a(layers[li + 1], 1)
                else:
                    finish_tile(1)
                    if u + 1 < n_units:
                        phase_a(layers[0], 0)
                        phase_a(layers[0], 1)
        S.final_wait("pool", out_toks[-2:])

        with nc.Block() as block:
            def run(name):
                def body(e):
                    for waits, fn, inc_key, inc_amt in S.q[name]:
                        for k, c in waits:
                            e.wait_ge(sems[k], c)
                        if fn is not None:
                            ins = fn(e)
                            if inc_key is not None:
                                ins.then_inc(sems[inc_key], inc_amt)
                return body
            block.tensor(run("pe"))
            block.scalar(run("act"))
            block.vector(run("dve"))
            block.gpsimd(run("pool"))
            block.sync(run("sp"))
    return nc


def _host_consts():
    ident = np.eye(128, dtype=np.float32)
    s = np.arange(128)[:, None]
    t = np.arange(128)[None, :]
    mask = (t >= s).astype(np.float32)
    coef = np.zeros((4, 16), np.float32)
    for g in range(4):
        win = 2 ** (g + 1)
        for tt in range(16):
            if tt < win - 1:
                coef[g, tt] = 1.0 / (tt + 1) - 1.0 / win
    coefb = np.broadcast_to(coef.reshape(1, 64), (128, 64))
    return np.ascontiguousarray(np.concatenate([ident, mask, coefb], axis=1), dtype=np.float32)


def _host_params(norm_g, pool_scale, conv_w, conv_b, conv_ln_g, conv_ln_b, sc_w, final_g):
    pp = np.zeros((128, NPP), np.float32)
    for l in range(2):
        o = l * PP_L
        pp[:, o + PP_NORMG:o + PP_NORMG + 8] = norm_g[l].reshape(8, 128).T
        pp[:, o + PP_PSCALE:o + PP_PSCALE + 4] = pool_scale[l].reshape(4, 128).T
        pp[:, o + PP_CONVB:o + PP_CONVB + 4] = conv_b[l].reshape(4, 128).T
        pp[:, o + PP_CLNG:o + PP_CLNG + 4] = conv_ln_g[l].reshape(4, 128).T
        pp[:, o + PP_CLNB:o + PP_CLNB + 4] = conv_ln_b[l].reshape(4, 128).T
        cw = conv_w[l].reshape(CONV_K, 4, 128)
        pp[:, o + PP_CONVW:o + PP_CONVW + 124] = cw.transpose(2, 1, 0).reshape(128, 124)
        sw = sc_w[l].reshape(3, 4, 128)
        pp[:, o + PP_SCW:o + PP_SCW + 12] = sw.transpose(2, 1, 0).reshape(128, 12)
    pp[:, PP_FINAL:PP_FINAL + 8] = final_g.reshape(8, 128).T
    return pp


_NC_CACHE = {}


def _get_nc(layers, final, n_units):
    key = (tuple(layers), final, n_units)
    if key not in _NC_CACHE:
        _NC_CACHE[key] = _build(list(layers), final, n_units)
    return _NC_CACHE[key]


def _common_maps(w_in, w_branch, w_o, norm_g, pool_w, pool_scale, conv_w, conv_b, conv_ln_g, conv_ln_b,
                 sgu_ln_g, sgu_ln_b, sgu_w, sgu_b, sc_w, final_g):
    f = lambda a: np.ascontiguousarray(np.asarray(a), dtype=np.float32)
    pp = _host_params(f(norm_g), f(pool_scale), f(conv_w), f(conv_b), f(conv_ln_g), f(conv_ln_b), f(sc_w), f(final_g))
    bcv = np.stack([np.stack([f(sgu_ln_g)[l], f(sgu_ln_b)[l]]) for l in range(2)]).reshape(1, 4 * 512)
    bcv = np.ascontiguousarray(np.broadcast_to(bcv, (128, 4 * 512)))
    poolw = np.ascontiguousarray(f(pool_w).reshape(8, 128, 128).transpose(1, 0, 2).reshape(128, 8 * 128))
    sguw = np.ascontiguousarray(f(sgu_w).reshape(8, 128, 128).transpose(2, 0, 1).reshape(128, 8 * 128))
    sgub = np.ascontiguousarray(f(sgu_b).reshape(1, 1024))
    return {"w_in": f(w_in), "w_branch": f(w_branch), "w_o": f(w_o), "pp": pp, "bc": bcv,
            "poolw": poolw, "sguw": sguw, "sgub": sgub, "cst": _host_consts()}


def _run(xT_cores, common, layers, final, n_units):
    nc = _get_nc(layers, final, n_units)
    in_maps = []
    for i in range(NCORE):
        m = dict(common)
        m["xT"] = xT_cores[i]
        in_maps.append(m)
    res = run_bass_kernel_spmd(nc, in_maps, core_ids=list(range(NCORE)))
    if DEBUG:
        DBG_OUT["z"] = [np.asarray(r["dbgz"]) for r in res.results]
        DBG_OUT["h"] = [np.asarray(r["dbgh"]) for r in res.results]
        for k in ("dbg1", "dbg2", "dbg3", "dbg4"):
            DBG_OUT[k] = [np.asarray(r[k]) for r in res.results]
    return [np.asarray(r["outT"]) for r in res.results]


FUSED = True


def kernel(x, norm_g, w_in, pool_w, pool_scale, conv_w, conv_b, conv_ln_g, conv_ln_b,
           sgu_ln_g, sgu_ln_b, sgu_w, sgu_b, sc_w, w_branch, w_o, final_g):
    x = np.asarray(x, dtype=np.float32)
    B, S_, Dm = x.shape
    per = B // NCORE
    n_units = per * S_ // T
    common = _common_maps(w_in, w_branch, w_o, norm_g, pool_w, pool_scale, conv_w, conv_b, conv_ln_g, conv_ln_b,
                          sgu_ln_g, sgu_ln_b, sgu_w, sgu_b, sc_w, final_g)
    xT_cores = [np.ascontiguousarray(x[i * per:(i + 1) * per].reshape(per * S_, Dm).T) for i in range(NCORE)]
    if FUSED:
        outs = _run(xT_cores, common, (0, 1), True, n_units)
    else:
        mid = _run(xT_cores, common, (0,), False, n_units)
        outs = _run(mid, common, (1,), True, n_units)
    out = np.empty((B, S_, Dm), np.float32)
    for i in range(NCORE):
        out[i * per:(i + 1) * per] = outs[i].T.reshape(per, S_, Dm)
    return out
```
